# Optimizing a Trainium2 kernel written in Bass

```python
import jax, jax.numpy as jnp
from jax import lax
import numpy as np

D_MODEL = 1024
BATCH = 8
SEQ = 2048
DEPTH = 2
DEC_BATCH = 128
DEC_SEQ = 8
PAST_LEN = 16384
PAGE_SIZE = 128

H_A = 4
DV_A = D_MODEL // 2 // H_A
DK_A = DV_A // 2
GLA_LOWRANK = 16
GLA_TAU = 16.0
H_B = 4
DV_B = D_MODEL // 2 // H_B
DK_B = 128
H_C = 4
DK_C = D_MODEL // H_C
DV_C = 2 * DK_C
D_FF = ((8 * D_MODEL // 3 + 127) // 128) * 128
CHUNK = 64
ROPE_BASE = 10000.0
EPS = 1e-6
N_EVEN = (DEPTH + 1) // 2
N_ODD = DEPTH // 2
EVEN_SPLITS = (H_A * DK_A, H_A * DK_A, H_A * DV_A, H_A * DV_A, GLA_LOWRANK, H_B * DK_B, H_B * DK_B, H_B * DV_B, H_B * DV_B)
EVEN_IN = sum(EVEN_SPLITS)
ODD_SPLITS = (H_C * DK_C, H_C * DK_C, H_C * DV_C, H_C * DV_C)
ODD_IN = sum(ODD_SPLITS)

kernel_name = 'hybrid_gla_hgrn2_retnet_macaron_decode_step'


def rmsnorm(x, w):
    xf = x.astype(jnp.float32)
    y = xf * lax.rsqrt(jnp.mean(xf * xf, axis=-1, keepdims=True) + EPS)
    return (y * w.astype(jnp.float32)).astype(x.dtype)


def swiglu(x, w_gate, w_up, w_down):
    return (jax.nn.silu(x @ w_gate) * (x @ w_up)) @ w_down


def split_cols(p, sizes):
    return jnp.split(p, np.cumsum(sizes)[:-1].tolist(), axis=-1)


def to_heads(a, n_heads):
    b, t, _ = a.shape
    return a.reshape(b, t, n_heads, -1).transpose(0, 2, 1, 3)


def gated_head_norm(o, gate, w, dtype):
    o = o.astype(jnp.float32)
    y = o * lax.rsqrt(jnp.mean(o * o, axis=-1, keepdims=True) + EPS) * w.astype(jnp.float32) * jax.nn.silu(gate.astype(jnp.float32))
    b, h, t, dv = y.shape
    return y.transpose(0, 2, 1, 3).reshape(b, t, h * dv).astype(dtype)


def rotary(x, pos):
    half = x.shape[-1] // 2
    inv_freq = ROPE_BASE ** (-jnp.arange(half, dtype=jnp.float32) / half)
    ang = pos.astype(jnp.float32)[:, None] * inv_freq[None, :]
    cos, sin = jnp.cos(ang), jnp.sin(ang)
    x1, x2 = x[..., :half], x[..., half:]
    return jnp.concatenate([x1 * cos - x2 * sin, x1 * sin + x2 * cos], axis=-1)


def chunked_decay_attention(q, k, v, log_decay, s0):
    bsz, nh, t, dk = q.shape
    dv = v.shape[-1]
    dg = log_decay.shape[-1]
    c = min(CHUNK, t)
    n = -(-t // c)
    pad = n * c - t
    q, k, v, g = (a.astype(jnp.float32) for a in (q, k, v, log_decay))
    if pad:
        q, k, v, g = (jnp.pad(a, ((0, 0), (0, 0), (0, pad), (0, 0))) for a in (q, k, v, g))

    def to_chunks(a):
        return jnp.moveaxis(a.reshape(bsz, nh, n, c, a.shape[-1]), 2, 0)

    causal = jnp.tril(jnp.ones((c, c), dtype=bool))

    def step(s, blk):
        qc, kc, vc, gc = blk
        b = jnp.cumsum(gc, axis=2)
        diff = b[:, :, :, None, :] - b[:, :, None, :, :]
        dec = jnp.exp(jnp.where(causal[:, :, None], diff, -jnp.inf))
        if dg == 1:
            scores = jnp.einsum('bhtd,bhsd->bhts', qc, kc) * dec[..., 0]
        else:
            scores = jnp.einsum('bhtd,bhsd,bhtsd->bhts', qc, kc, dec)
        o = jnp.einsum('bhts,bhse->bhte', scores, vc) + jnp.einsum('bhtd,bhde->bhte', qc * jnp.exp(b), s)
        b_last = b[:, :, -1:, :]
        s_new = jnp.exp(b_last[:, :, 0, :])[..., None] * s + jnp.einsum('bhsd,bhse->bhde', kc * jnp.exp(b_last - b), vc)
        return s_new, o

    s_fin, o = lax.scan(step, s0.astype(jnp.float32), (to_chunks(q), to_chunks(k), to_chunks(v), to_chunks(g)))
    o = jnp.moveaxis(o, 0, 2).reshape(bsz, nh, n * c, dv)[:, :, :t]
    return o, s_fin.astype(s0.dtype)


def even_mixer(h, s_gla, s_hgrn, w_in, w_gate2, b_gate, gla_norm_w, lb, hgrn_norm_w, w_out):
    q_a, k_a, v_a, r_a, lr_a, q_b, f_b, i_b, g_b = split_cols(h @ w_in, EVEN_SPLITS)
    log_alpha = jax.nn.log_sigmoid((lr_a @ w_gate2 + b_gate).astype(jnp.float32)) / GLA_TAU
    o_a, s_gla_new = chunked_decay_attention(to_heads(q_a, H_A) * (DK_A ** -0.5), to_heads(k_a, H_A), to_heads(v_a, H_A), to_heads(log_alpha, H_A), s_gla)
    y_a = gated_head_norm(o_a, to_heads(r_a, H_A), gla_norm_w, h.dtype)
    lb_h = lb.astype(jnp.float32).reshape(H_B, 1, DK_B)
    f = lb_h + (1.0 - lb_h) * jax.nn.sigmoid(to_heads(f_b, H_B).astype(jnp.float32))
    o_b, s_hgrn_new = chunked_decay_attention(jax.nn.silu(to_heads(q_b, H_B)), 1.0 - f, to_heads(i_b, H_B), jnp.log(f), s_hgrn)
    y_b = gated_head_norm(o_b, to_heads(g_b, H_B), hgrn_norm_w, h.dtype)
    return jnp.concatenate([y_a, y_b], axis=-1) @ w_out, s_gla_new, s_hgrn_new


def odd_mixer(h, pos, s_ret, w_in, ret_norm_w, w_out):
    q, k, v, g = split_cols(h @ w_in, ODD_SPLITS)
    q = rotary(to_heads(q, H_C).astype(jnp.float32), pos)
    k = rotary(to_heads(k, H_C).astype(jnp.float32), pos) * (DK_C ** -0.5)
    b, t, _ = h.shape
    log_gamma = jnp.log(1.0 - 2.0 ** (-5.0 - jnp.arange(H_C, dtype=jnp.float32)))
    ld = jnp.broadcast_to(log_gamma[None, :, None, None], (b, H_C, t, 1))
    o, s_new = chunked_decay_attention(q, k, to_heads(v, H_C), ld, s_ret)
    return gated_head_norm(o, to_heads(g, H_C), ret_norm_w, h.dtype) @ w_out, s_new


def trunk(x, pos, s_gla, s_hgrn, s_ret, norm_w, ffn_w_gate, ffn_w_up, ffn_w_down, even_w_in, gla_w_gate2, gla_b_gate, gla_norm_w, hgrn_lb_table, hgrn_norm_w, even_w_out, odd_w_in, ret_norm_w, odd_w_out):
    lb_all = jnp.cumsum(jax.nn.softmax(hgrn_lb_table.astype(jnp.float32), axis=0), axis=0)
    new_gla, new_hgrn, new_ret = [], [], []
    for l in range(DEPTH):
        nw = norm_w[l]
        x = x + 0.5 * rmsnorm(swiglu(rmsnorm(x, nw[0]), ffn_w_gate[l, 0], ffn_w_up[l, 0], ffn_w_down[l, 0]), nw[1])
        hn = rmsnorm(x, nw[2])
        if l % 2 == 0:
            e = l // 2
            m, sg, sh = even_mixer(hn, s_gla[e], s_hgrn[e], even_w_in[e], gla_w_gate2[e], gla_b_gate[e], gla_norm_w[e], lb_all[l], hgrn_norm_w[e], even_w_out[e])
            new_gla.append(sg)
            new_hgrn.append(sh)
        else:
            o = l // 2
            m, sr = odd_mixer(hn, pos, s_ret[o], odd_w_in[o], ret_norm_w[o], odd_w_out[o])
            new_ret.append(sr)
        x = x + rmsnorm(m, nw[3])
        x = x + 0.5 * rmsnorm(swiglu(rmsnorm(x, nw[4]), ffn_w_gate[l, 1], ffn_w_up[l, 1], ffn_w_down[l, 1]), nw[5])
    return x, jnp.stack(new_gla), jnp.stack(new_hgrn), jnp.stack(new_ret)


def setup_inputs(seed: int = 0) -> dict:
    key = jax.random.key(seed)
    ks = jax.random.split(key, 20)

    def nrm(k, shape, scale):
        return jax.random.normal(k, shape, jnp.float32) * scale

    return {
        'x_prompt': nrm(ks[0], (BATCH, SEQ, D_MODEL), 1.0),
        'x_sample': nrm(ks[1], (DEC_BATCH, DEC_SEQ, D_MODEL), 1.0),
        'state_gla': nrm(ks[2], (N_EVEN, DEC_BATCH, H_A, DK_A, DV_A), 0.5),
        'state_hgrn': nrm(ks[3], (N_EVEN, DEC_BATCH, H_B, DK_B, DV_B), 0.5),
        'state_ret': nrm(ks[4], (N_ODD, DEC_BATCH, H_C, DK_C, DV_C), 0.5),
        'norm_w': 1.0 + nrm(ks[5], (DEPTH, 6, D_MODEL), 0.05),
        'ffn_w_gate': nrm(ks[6], (DEPTH, 2, D_MODEL, D_FF), D_MODEL ** -0.5),
        'ffn_w_up': nrm(ks[7], (DEPTH, 2, D_MODEL, D_FF), D_MODEL ** -0.5),
        'ffn_w_down': nrm(ks[8], (DEPTH, 2, D_FF, D_MODEL), D_FF ** -0.5),
        'even_w_in': nrm(ks[9], (N_EVEN, D_MODEL, EVEN_IN), D_MODEL ** -0.5),
        'gla_w_gate2': nrm(ks[10], (N_EVEN, GLA_LOWRANK, H_A * DK_A), GLA_LOWRANK ** -0.5),
        'gla_b_gate': nrm(ks[11], (N_EVEN, H_A * DK_A), 0.1),
        'gla_norm_w': 1.0 + nrm(ks[12], (N_EVEN, DV_A), 0.05),
        'hgrn_lb_table': nrm(ks[13], (DEPTH + 1, H_B * DK_B), 0.5),
        'hgrn_norm_w': 1.0 + nrm(ks[14], (N_EVEN, DV_B), 0.05),
        'even_w_out': nrm(ks[15], (N_EVEN, H_A * DV_A + H_B * DV_B, D_MODEL), (H_A * DV_A + H_B * DV_B) ** -0.5),
        'odd_w_in': nrm(ks[16], (N_ODD, D_MODEL, ODD_IN), D_MODEL ** -0.5),
        'ret_norm_w': 1.0 + nrm(ks[17], (N_ODD, DV_C), 0.05),
        'odd_w_out': nrm(ks[18], (N_ODD, H_C * DV_C, D_MODEL), (H_C * DV_C) ** -0.5),
    }


def reference(x_prompt, x_sample, state_gla, state_hgrn, state_ret, norm_w, ffn_w_gate, ffn_w_up, ffn_w_down, even_w_in, gla_w_gate2, gla_b_gate, gla_norm_w, hgrn_lb_table, hgrn_norm_w, even_w_out, odd_w_in, ret_norm_w, odd_w_out):
    bp, tp, _ = x_prompt.shape
    ts = x_sample.shape[1]
    pos_p = jnp.arange(tp, dtype=jnp.int32)
    pos_s = PAST_LEN + jnp.arange(ts, dtype=jnp.int32)
    z_gla = jnp.zeros((N_EVEN, bp, H_A, DK_A, DV_A), x_prompt.dtype)
    z_hgrn = jnp.zeros((N_EVEN, bp, H_B, DK_B, DV_B), x_prompt.dtype)
    z_ret = jnp.zeros((N_ODD, bp, H_C, DK_C, DV_C), x_prompt.dtype)
    y_prompt, gla_p, hgrn_p, ret_p = trunk(x_prompt, pos_p, z_gla, z_hgrn, z_ret, norm_w, ffn_w_gate, ffn_w_up, ffn_w_down, even_w_in, gla_w_gate2, gla_b_gate, gla_norm_w, hgrn_lb_table, hgrn_norm_w, even_w_out, odd_w_in, ret_norm_w, odd_w_out)
    y_sample, gla_s, hgrn_s, ret_s = trunk(x_sample, pos_s, state_gla, state_hgrn, state_ret, norm_w, ffn_w_gate, ffn_w_up, ffn_w_down, even_w_in, gla_w_gate2, gla_b_gate, gla_norm_w, hgrn_lb_table, hgrn_norm_w, even_w_out, odd_w_in, ret_norm_w, odd_w_out)
    return (y_prompt, y_sample, gla_p, hgrn_p, ret_p, gla_s, hgrn_s, ret_s)
```

```python
import numpy as np
import concourse.bass as bass
import concourse.mybir as mybir
from concourse.bass_utils import run_bass_kernel_spmd

F32, BF16 = mybir.dt.float32, mybir.dt.bfloat16
AF = mybir.ActivationFunctionType
ALU = mybir.AluOpType

D = 1024
NTOK = 2176
DFF = 2816
PAST = 16384
EPS = 1e-6
NCORE = 8
GROUP_TILES = [[0, 1, 2, 3, 4, 5], [6, 7, 8, 9, 10, 11], [12, 13, 14, 15, 16]]
NTM = 768
GAMMA = [1.0 - 2.0 ** (-5.0 - h) for h in range(4)]
DBG = 99
DBGB = 0
DBG2 = 99


def _caller_tag():
    import sys
    f = sys._getframe(2)
    names = []
    while f is not None and len(names) < 4:
        n = f.f_code.co_name
        if n not in ("op", "dma", "mm", "tr", "act", "tt", "ts", "stt", "cpy", "dma_sp", "dma_pool", "<lambda>"):
            names.append(n)
        f = f.f_back
    return "/".join(names[:2])


class R:
    __slots__ = ("name", "w", "rs")

    def __init__(self, name):
        self.name = name
        self.w = None
        self.rs = {}


REORDER = True
RES_ENG = 'dve'
WINDOW = 512
PRIO = 0
SLACK = 0.5
FENCED = ("pe", "act", "dve")


class Sched:
    ENG = ["pe", "act", "dve", "pool", "sp"]

    def __init__(self, nc):
        self.nc = nc
        self.ops = {e: [] for e in self.ENG}
        self.dsem = {}
        self.dtok = {}
        self.esem = {}
        self.seg = 0

    def _deps(self, reads, writes):
        deps = {}
        for r in reads:
            if r.w is not None:
                deps[r.w] = "raw"
        for r in writes:
            if r.w is not None:
                deps.setdefault(r.w, "waw")
            for t in r.rs:
                deps.setdefault(t, "war")
        return list(deps.items())

    def _post(self, tok, reads, writes):
        for r in reads:
            r.rs[tok] = 1
        for r in writes:
            r.w = tok
            r.rs = {}

    def op(self, eng, fn, reads=(), writes=(), cost=0.2, tset=0):
        deps = self._deps(reads, writes)
        idx = len(self.ops[eng])
        tok = ("c", eng, idx)
        self.ops[eng].append(dict(fn=fn, deps=deps, dma=None, cost=cost, seg=self.seg, idx=idx, tset=tset,
                                  tag=_caller_tag()))
        self._post(tok, reads, writes)
        return tok

    def dma(self, eng, fn, key, reads=(), writes=(), nbytes=65536):
        deps = self._deps(reads, writes)
        if key not in self.dsem:
            self.dsem[key] = [self.nc.alloc_semaphore("d_" + key), 0, eng]
        h = self.dsem[key]
        h[1] += 16
        tok = ("d", key, h[1])
        idx = len(self.ops[eng])
        self.dtok[(key, h[1])] = (eng, idx)
        self.ops[eng].append(dict(fn=fn, deps=deps, dma=key, cost=0.0, seg=self.seg, idx=idx, tset=0, nbytes=nbytes,
                                  dval=h[1], tag=_caller_tag()))
        self._post(tok, reads, writes)
        return tok

    def barrier(self, engs=None):
        self.seg += 1

    def reorder(self):
        ENG = self.ENG
        ops = self.ops
        node_deps = {}
        succ = {}
        for e in ENG:
            for i, o in enumerate(ops[e]):
                ds = set()
                for tok, kind in o["deps"]:
                    nd = (tok[1], tok[2]) if tok[0] == "c" else self.dtok[(tok[1], tok[2])]
                    if nd != (e, i):
                        ds.add(nd)
                node_deps[(e, i)] = ds
                for nd in ds:
                    succ.setdefault(nd, []).append((e, i))
        n_un = {k: len(v) for k, v in node_deps.items()}
        tail = {}
        for e in ENG:
            for i in range(len(ops[e])):
                tail[(e, i)] = None
        import sys
        order_nodes = []
        indeg = dict(n_un)
        stack = [k for k, n in indeg.items() if n == 0]
        while stack:
            k = stack.pop()
            order_nodes.append(k)
            for nd in succ.get(k, ()):
                indeg[nd] -= 1
                if indeg[nd] == 0:
                    stack.append(nd)
        for k in reversed(order_nodes):
            o = ops[k[0]][k[1]]
            c = o["cost"] if o["dma"] is None else 2.5 + o["nbytes"] / 3.0e5
            t = 0.0
            for nd in succ.get(k, ()):
                if tail[nd] is not None and tail[nd] > t:
                    t = tail[nd]
            tail[k] = c + t
        done = {}
        ready_t = {}
        t_eng = {e: 0.0 for e in ENG}
        cur_set = 0
        scheduled = {e: [False] * len(ops[e]) for e in ENG}
        ptr = {e: 0 for e in ENG}
        ready = {e: set() for e in ENG}
        order = {e: [] for e in ENG}
        for k, n in n_un.items():
            if n == 0:
                ready[k[0]].add(k[1])
                ready_t[k] = 0.0
        seg_left = {}
        for e in FENCED:
            for o in ops[e]:
                seg_left[o["seg"]] = seg_left.get(o["seg"], 0) + 1
        segs = sorted(seg_left)
        cur_seg_i = 0
        fence_t = 0.0
        max_done_fenced = 0.0
        total = sum(len(ops[e]) for e in ENG)
        nsched = 0
        LAT = 0.1
        while nsched < total:
            while cur_seg_i < len(segs) and seg_left[segs[cur_seg_i]] == 0:
                cur_seg_i += 1
                fence_t = max_done_fenced
            cur_seg = segs[cur_seg_i] if cur_seg_i < len(segs) else None
            best = None
            for e in ENG:
                lim = ptr[e] + WINDOW
                for i in ready[e]:
                    if i >= lim:
                        continue
                    o = ops[e][i]
                    st = max(t_eng[e], ready_t[(e, i)])
                    if e in FENCED:
                        if o["seg"] != cur_seg:
                            continue
                        st = max(st, fence_t)
                    eff = st
                    if e == "act" and o["tset"] and o["tset"] != cur_set:
                        eff = st + 1.5
                    if PRIO == 0:
                        key = (eff, i, e)
                    else:
                        key = (round(eff / SLACK), -tail[(e, i)], eff, i)
                    if best is None or key < best[0]:
                        best = (key, e, i, st, eff)
            assert best is not None, "scheduler stuck"
            _, e, i, st, eff = best
            o = ops[e][i]
            if e == "act" and o["tset"]:
                cur_set = o["tset"]
            st = eff
            if o["dma"] is not None:
                issue = 1.1 if e == "pool" else 0.12
                t_eng[e] = st + issue
                dn = st + issue + 2.0 + o["nbytes"] / 3.0e5
            else:
                dn = st + o["cost"]
                t_eng[e] = dn
            done[(e, i)] = dn
            if e in FENCED:
                seg_left[o["seg"]] -= 1
                max_done_fenced = max(max_done_fenced, dn)
            scheduled[e][i] = True
            ready[e].discard(i)
            order[e].append(i)
            nsched += 1
            while ptr[e] < len(ops[e]) and scheduled[e][ptr[e]]:
                ptr[e] += 1
            for nd in succ.get((e, i), ()):
                n_un[nd] -= 1
                if n_un[nd] == 0:
                    rt = 0.0
                    for d in node_deps[nd]:
                        rt = max(rt, done[d] + (0.0 if d[0] == nd[0] else LAT))
                    ready_t[nd] = rt
                    ready[nd[0]].add(nd[1])
        for e in ("pool", "sp"):
            last = {}
            for i in order[e]:
                o = ops[e][i]
                if o["dma"] is None:
                    continue
                assert o["dval"] > last.get(o["dma"], 0), "dma order changed for " + o["dma"]
                last[o["dma"]] = o["dval"]
        self.sim_time = max(done.values())
        return order

    def finalize(self):
        for e in self.ENG:
            self.esem[e] = self.nc.alloc_semaphore("e_" + e)
        if REORDER:
            order = self.reorder()
        else:
            order = {e: list(range(len(self.ops[e]))) for e in self.ENG}
        self.order = order
        pos = {e: {i: p for p, i in enumerate(order[e])} for e in self.ENG}
        seg_last = {e: {} for e in FENCED}
        for e in FENCED:
            for i in order[e]:
                seg_last[e][self.ops[e][i]["seg"]] = i
        marked = {e: set() for e in self.ENG}
        for e in self.ENG:
            prev_seg = 0
            for i in order[e]:
                o = self.ops[e][i]
                cw, dw = {}, {}
                for tok, kind in o["deps"]:
                    if tok[0] == "c":
                        _, src, idx = tok
                        if src == e == "pe":
                            continue
                        if src not in cw or pos[src][idx] > pos[src][cw[src]]:
                            cw[src] = idx
                    else:
                        _, key, val = tok
                        if val > dw.get(key, 0):
                            dw[key] = val
                if e in FENCED and o["seg"] != prev_seg:
                    for e2 in FENCED:
                        cands = [sg for sg in seg_last[e2] if sg < o["seg"]]
                        if cands:
                            j = seg_last[e2][max(cands)]
                            if e2 not in cw or pos[e2][j] > pos[e2][cw[e2]]:
                                cw[e2] = j
                    prev_seg = o["seg"]
                o["cw"], o["dw"] = cw, dw
                for src, idx in cw.items():
                    marked[src].add(idx)
        self.rank = {}
        for e in self.ENG:
            r = {}
            c = 0
            for i in order[e]:
                if i in marked[e]:
                    c += 1
                    r[i] = c
            self.rank[e] = r
        self.marked = marked

    def emit(self, ename, eng):
        seen = {}
        ops = self.ops[ename]
        for i in self.order[ename]:
            o = ops[i]
            for src, idx in o["cw"].items():
                val = self.rank[src][idx]
                if seen.get(("c", src), 0) < val:
                    seen[("c", src)] = val
                    eng.wait_ge(self.esem[src], val)
            for key, val in o["dw"].items():
                if seen.get(("d", key), 0) < val:
                    seen[("d", key)] = val
                    eng.wait_ge(self.dsem[key][0], val)
            ins = o["fn"](eng)
            if o["dma"] is not None:
                ins.then_inc(self.dsem[o["dma"]][0], 16)
            elif i in self.marked[ename]:
                ins.then_inc(self.esem[ename], 1)
        for key, (sem, total, e) in self.dsem.items():
            if e == ename:
                eng.wait_ge(sem, total)


def _consts():
    c = {}
    c["ident"] = np.eye(128, dtype=np.float32)
    c["ones"] = np.ones((128, 128), dtype=np.float32)
    pos = np.concatenate([np.arange(2048), np.tile(PAST + np.arange(8), 16)]).astype(np.float32)
    inv = (np.float32(10000.0) ** (-np.arange(128, dtype=np.float32) / np.float32(128))).astype(np.float32)
    ang = (inv[:, None] * pos[None, :]).astype(np.float32)
    c["rope"] = np.stack([np.cos(ang), np.sin(ang)], axis=1).astype(np.float32)
    s = np.arange(128)[:, None]
    t = np.arange(128)[None, :]
    mp = (s <= t)
    ms = (s // 8 == t // 8) & (s <= t)
    c["mask0"] = np.stack([mp, ms], axis=1).astype(np.float32)
    tt = np.arange(NTOK)
    sm = np.where(tt < 2048, (tt % 128 != 0), (tt % 8 != 0)).astype(np.float32)
    c["scanm"] = np.broadcast_to(sm[None, :], (128, NTOK)).copy()
    dqt = np.zeros((128, 4, 2, 128), np.float64)
    dmt = np.zeros((128, 4, 2, 128), np.float64)
    dkc = np.zeros((128, 4, 2), np.float64)
    for h in range(4):
        g = GAMMA[h]
        dqt[:, h, 0, :] = (g ** (np.arange(128) + 1.0))[None, :]
        dqt[:, h, 1, :] = (g ** ((np.arange(128) % 8) + 1.0))[None, :]
        dmt[:, h, 0, :] = np.where(mp, (g ** (-(s + 1.0))) / 16.0, 0.0)
        dmt[:, h, 1, :] = np.where(ms, (g ** (-((s % 8) + 1.0))) / 16.0, 0.0)
        dkc[:, h, 0] = (g ** (127.0 - np.arange(128))) / 16.0
        dkc[:, h, 1] = (g ** (7.0 - (np.arange(128) % 8))) / 16.0
    c["dqt"] = dqt.astype(np.float32)
    c["dmt"] = dmt.astype(np.float32)
    c["dkc"] = dkc.astype(np.float32)
    c["bm"] = (np.arange(128)[:, None] // 8 == np.arange(16)[None, :]).astype(np.float32)
    c["hm"] = ((np.arange(128)[:, None] // 64 == np.arange(2)[None, :]) * 0.125).astype(np.float32)
    return c


CONST_SHAPES = dict(ident=[128, 128], ones=[128, 128], rope=[128, 2, NTOK], mask0=[128, 2, 128],
                    scanm=[128, NTOK], dqt=[128, 4, 2, 128], dmt=[128, 4, 2, 128], dkc=[128, 4, 2], bm=[128, 16], hm=[128, 2])

IN_SHAPES = dict(
    xT=[D, NTOK], sgla=[16, 4, 64, 128], shg=[16, 4, 128, 128], sret=[16, 4, 256, 512],
    nw=[128, 96], wg=[2, 2, D, DFF], wu=[2, 2, D, DFF], wd=[2, 2, DFF, D],
    ewin=[D, 3600], w2=[16, 256], bg=[128, 2], glanw=[128, 1], lbt=[128, 3, 4], hgnw=[128, 1],
    ewout=[D, D], owin=[D, 6144], retnw=[128, 4], owout=[2048, D],
)
OUT_SHAPES = dict(yT=[D, NTOK], gla_p=[4, 64, 128], hgrn_p=[4, 128, 128], ret_p=[4, 256, 512],
                  gla_s=[16, 4, 64, 128], hgrn_s=[16, 4, 128, 128], ret_s=[16, 4, 256, 512])


def build(stop=99):
    nc = bass.Bass("TRN2", target_bir_lowering=False)
    S = Sched(nc)
    dr = {}
    for k, shp in IN_SHAPES.items():
        dr[k] = nc.dram_tensor(k, shp, F32, kind="ExternalInput").ap()
    for k, shp in CONST_SHAPES.items():
        dr[k] = nc.dram_tensor("c_" + k, shp, F32, kind="ExternalInput").ap()
    for k, shp in OUT_SHAPES.items():
        dr[k] = nc.dram_tensor(k, shp, F32, kind="ExternalOutput").ap()
    dr["scr_ret"] = nc.dram_tensor("scr_ret", [4, 256, 512], F32, kind="Internal").ap()
    R_scr = [R("scr%d" % h) for h in range(4)]

    def sb(name, shape, dt):
        return nc.alloc_sbuf_tensor(name, shape, dt)

    X = sb("X", [128, 8, NTM], F32)
    XN = sb("XN", [128, 8, NTM], BF16)
    HB = sb("HB", [128, 16, NTM], BF16)
    YB = sb("YB", [128, 8, NTM], F32)
    class RB:
        def __init__(self, name):
            self.b = [R(name + "a"), R(name + "b")]

        def cols(self, c0, c1):
            out = []
            if c0 < 384:
                out.append(self.b[0])
            if c1 > 384:
                out.append(self.b[1])
            return out

        def all(self):
            return list(self.b)

    def flat(lst):
        return [r for x in lst for r in x.all()]

    RX = [RB("X%d" % i) for i in range(8)]
    RXN = [RB("XN%d" % i) for i in range(8)]
    RHB = [RB("HB%d" % i) for i in range(16)]
    RY = [RB("Y%d" % i) for i in range(8)]
    ybf = YB[:, :, :].rearrange("p a n -> p (a n)")
    FS = ybf[:, 0:4 * NTM].rearrange("p (a n) -> p a n", a=4)
    BS = ybf[:, 4 * NTM:8 * NTM].bitcast(BF16).rearrange("p (a n) -> p a n", a=8)
    R_FS = [R("FS%d" % i) for i in range(4)]
    R_BS = [R("BS%d" % i) for i in range(8)]
    KDB = sb("KDB", [128, 4, NTM], BF16)
    R_KDB = [R("KDB%d" % i) for i in range(4)]
    ROPE = KDB[:, :, :].rearrange("p a n -> p (a n)").bitcast(F32).rearrange("p (a n) -> p a n", a=2)
    VT2 = [sb("VT%d" % i, [128, 6, 512], BF16) for i in range(2)]
    R_VT2 = [[R("VT%d_%d" % (i, j)) for j in range(6)] for i in range(2)]
    VT, R_VT = VT2[0], R_VT2[0]
    KDT = sb("KDT", [128, 6, 512], BF16)
    R_KDT = [R("KDT%d" % i) for i in range(6)]
    R_KDT2 = [R_KDT, [R("KDTb%d" % i) for i in range(6)]]
    LR = sb("LR", [16, NTM], BF16)
    R_LR = R("LR")
    RS = sb("RS", [128, NTM], F32)
    R_RS = R("RS")
    R_RSb = RB("RSb")
    R_LNTb = RB("LNTb")
    LNT = sb("LNT", [128, NTM], F32)
    R_LNT = R("LNT")
    TMP = [sb("TMP%d" % i, [128, 512], F32) for i in range(2)]
    R_TMP = [R("TMP%d" % i) for i in range(2)]
    SGT = [sb("SGT%d" % i, [128, 384], F32) for i in range(2)]
    R_SGT = [R("SGT%d" % i) for i in range(2)]
    PT = [sb("PT%d" % i, [128, 512], BF16) for i in range(2)]
    R_PT = [R("PT%d" % i) for i in range(2)]
    SQ = sb("SQ", [128, 512], BF16)
    R_SQ = R("SQ")
    RT = sb("RT", [128, 128], F32)
    R_RT = R("RT")
    BC = sb("BC", [128, 3, 8], F32)
    R_BC = R("BC")
    EL = sb("EL", [128, 4, 16], F32)
    R_EL = [R("EL%d" % i) for i in range(4)]
    VX = [sb("VX%d" % i, [128, 512], BF16) for i in range(2)]
    R_VX = [R("VX%d" % i) for i in range(2)]
    SA = sb("SA", [128, 2, 128], F32)
    SAb = sb("SAb", [128, 2, 128], BF16)
    SBs = sb("SBs", [128, 4, 128], F32)
    SBb = sb("SBb", [128, 4, 128], BF16)
    R_SA, R_SAb, R_SB, R_SBb = R("SA"), R("SAb"), R("SB"), R("SBb")
    SAb2 = [SAb, sb("SAbx", [128, 2, 128], BF16)]
    SBb2 = [SBb, sb("SBbx", [128, 4, 128], BF16)]
    R_SAb2 = [R_SAb, R("SAbx")]
    R_SBb2 = [R_SBb, R("SBbx")]
    SR = sb("SR", [128, 2, 512], F32)
    SRB = [sb("SRb%d" % i, [128, 2, 512], BF16) for i in range(2)]
    R_SR = R("SR")
    R_SRB = [R("SRb%d" % i) for i in range(2)]
    NS0 = 3
    S0F = [sb("S0F%d" % i, [128, 2, 512], F32) for i in range(NS0)]
    S0B = [sb("S0B%d" % i, [128, 2, 512], BF16) for i in range(NS0)]
    R_S0F = [R("S0F%d" % i) for i in range(NS0)]
    R_S0B = [R("S0B%d" % i) for i in range(NS0)]
    WS = [sb("WS%d" % i, [128, 8, 256], BF16) for i in range(4)]
    R_WS = [R("WS%d" % i) for i in range(4)]
    R_WSu = [R("WSu%d" % i) for i in range(4)]
    NWD = 4
    WD = [sb("WD%d" % i, [128, 16, 128], BF16) for i in range(NWD)]
    R_WD = [R("WD%d" % i) for i in range(NWD)]
    wsc = [0]
    wdc = [0]
    IDENT = sb("IDENT", [128, 128], BF16)
    ONES = sb("ONES", [128, 128], BF16)
    MASK0 = sb("MASK0", [128, 2, 128], F32)
    SCANM = sb("SCANM", [128, NTM], F32)
    DQT = [sb("DQT%d" % i, [128, 2, 128], F32) for i in range(2)]
    DMT = [sb("DMT%d" % i, [128, 2, 128], F32) for i in range(2)]
    R_DQ = [R("DQ%d" % i) for i in range(2)]
    DKC = sb("DKC", [128, 4, 2], F32)
    BM = sb("BM", [128, 16], F32)
    HM = sb("HM", [128, 2], F32)
    NW = sb("NW", [128, 96], F32)
    NWH = sb("NWH", [128, 96], F32)
    W2 = sb("W2", [16, 256], BF16)
    NBG = sb("NBG", [128, 2], F32)
    GLANW = sb("GLANW", [128, 1], F32)
    HGNW = sb("HGNW", [128, 1], F32)
    RETNW = sb("RETNW", [128, 4], F32)
    LBT = sb("LBT", [128, 3, 4], F32)
    LB = sb("LB", [128, 4], F32)
    OML = sb("OML", [128, 4], F32)
    R_C = R("consts")
    R_ROPE, R_SCANM = R("rope"), R("scanm")
    PS = [nc.alloc_psum_tensor("PS%d" % i, [128, 512], F32) for i in range(7)]
    PSTb = nc.alloc_psum_tensor("PSTb", [128, 1024], BF16)
    R_PS = [R("PS%d" % i) for i in range(7)]
    R_PST = R("PST")
    PSA, R_PSA = PS[0:2], R_PS[0:2]
    PSB, R_PSB = PS[2:4], R_PS[2:4]
    PSC, R_PSC = PS[4:6], R_PS[4:6]
    PSN, R_PSN = PS[6], R_PS[6]

    def fsz(ap):
        n = 1
        for d in ap.shape[1:]:
            n *= int(d)
        return n

    def dma_sp(out, in_, key, reads=(), writes=()):
        nb = fsz(out) * int(out.shape[0]) * 4
        return S.dma("sp", lambda e, o=out, i=in_: e.dma_start(out=o, in_=i), key, reads, writes, nbytes=nb)

    def dma_pool(out, in_, key, reads=(), writes=()):
        nb = fsz(out) * int(out.shape[0]) * 4
        return S.dma("pool", lambda e, o=out, i=in_: e.dma_start(out=o, in_=i), key, reads, writes, nbytes=nb)

    def mm(out, lhsT, rhs, start, stop, reads, writes):
        return S.op("pe", lambda e, o=out, l=lhsT, r=rhs, a=start, b=stop: e.matmul(o, l, r, start=a, stop=b),
                    reads, writes, cost=max(fsz(rhs), 48) / 2400.0 + 0.005)

    def tr(out, in_, reads, writes):
        return S.op("pe", lambda e, o=out, i=in_: e.transpose(o, i, IDENT[:, :]), reads, writes, cost=0.08)

    def act(out, in_, func, reads, writes, bias=None, scale=None):
        kw = {}
        if bias is not None:
            kw["bias"] = bias
        if scale is not None:
            kw["scale"] = scale
        tset = 1 if func == AF.Silu else (2 if func in (AF.Exp, AF.Ln) else 0)
        return S.op("act", lambda e, o=out, i=in_, f=func, k=kw: e.activation(out=o, in_=i, func=f, **k),
                    reads, writes, cost=fsz(out) / 1400.0 + 0.2, tset=tset)

    def tt(out, in0, in1, op, reads, writes, eng="dve"):
        return S.op(eng, lambda e, o=out, a=in0, b=in1, p=op: e.tensor_tensor(out=o, in0=a, in1=b, op=p),
                    reads, writes, cost=fsz(out) / 960.0 + 0.1)

    def ts(out, in0, s1, s2, op0, op1, reads, writes):
        if op1 is None:
            return S.op("dve", lambda e, o=out, a=in0, x=s1, p=op0: e.tensor_scalar(out=o, in0=a, scalar1=x,
                                                                                   scalar2=None, op0=p),
                        reads, writes, cost=fsz(out) / 960.0 + 0.1)
        return S.op("dve", lambda e, o=out, a=in0, x=s1, y=s2, p=op0, q=op1: e.tensor_scalar(
            out=o, in0=a, scalar1=x, scalar2=y, op0=p, op1=q), reads, writes, cost=fsz(out) / 960.0 + 0.1)

    def stt(out, in0, sc, in1, op0, op1, reads, writes):
        return S.op("dve", lambda e, o=out, a=in0, x=sc, b=in1, p=op0, q=op1: e.scalar_tensor_tensor(
            out=o, in0=a, scalar=x, in1=b, op0=p, op1=q), reads, writes, cost=fsz(out) / 960.0 + 0.1)

    def cpy(out, in_, reads, writes, eng="act"):
        if eng == "act":
            return act(out, in_, AF.Copy, reads, writes)
        return S.op(eng, lambda e, o=out, i=in_: e.tensor_copy(out=o, in_=i), reads, writes,
                    cost=fsz(out) / 960.0 + 0.1)

    def blocks(nt):
        return [(0, 384), (384, nt - 384)]

    R_CL = []
    for name, dst in [("rope", None), ("mask0", MASK0), ("dkc", DKC), ("bm", BM), ("hm", HM),
                      ("nw", NW), ("bg", NBG), ("glanw", GLANW), ("hgnw", HGNW), ("retnw", RETNW), ("lbt", LBT)]:
        if dst is None:
            continue
        src = dr[name]
        rc = R("c_" + name)
        R_CL.append(rc)
        dma_sp(dst[tuple(slice(None) for _ in dst.shape)], src, "const", writes=[rc])
    for nm_, dst_ in (("ident", IDENT), ("ones", ONES), ("w2", W2)):
        rc = R("c_" + nm_)
        R_CL.append(rc)
        dma_pool(dst_[:, :], dr[nm_], "constp", writes=[rc])
    JOIN = sb("JOIN", [128, 2], F32)
    S.op("dve", lambda e: e.memset(JOIN[:, :], 0.0), R_CL, [R_C])
    ts(NWH[:, :], NW[:, :], 0.5, None, ALU.mult, None, [R_C], [R_C])
    ts(NBG[:, :], NBG[:, :], -1.0, None, ALU.mult, None, [R_C], [R_C])
    act(LBT[:, :, :], LBT[:, :, :], AF.Exp, [R_C], [R_C])
    tt(LB[:, :], LBT[:, 0, :], LBT[:, 1, :], ALU.add, [R_C], [R_C])
    tt(LB[:, :], LB[:, :], LBT[:, 2, :], ALU.add, [R_C], [R_C])
    S.op("dve", lambda e: e.reciprocal(out=LB[:, :], in_=LB[:, :]), [R_C], [R_C])
    tt(LB[:, :], LB[:, :], LBT[:, 0, :], ALU.mult, [R_C], [R_C])
    ts(OML[:, :], LB[:, :], -1.0, 1.0, ALU.mult, ALU.add, [R_C], [R_C])

    for i in range(2):
        S.op("dve", lambda e, i=i: e.memset(PT[i][:, :], 0.0), [], [R_PT[i]])

    def nwi(l, i, kc):
        return (l * 6 + i) * 8 + kc

    def rms_stats(src_fn, rsrc, nt, nch, dsz):
        for (b0, bn) in blocks(nt):
            for kc in range(nch):
                act(XN[:, kc, b0:b0 + bn], src_fn(kc)[:, b0:b0 + bn], AF.Square, rsrc[kc].cols(b0, b0 + bn),
                    RXN[kc].cols(b0, b0 + bn))
        for (b0, bn) in blocks(nt):
            for kc in range(nch):
                mm(PSN[:, 0:bn], ONES[:, :], XN[:, kc, b0:b0 + bn], kc == 0, kc == nch - 1,
                   RXN[kc].cols(b0, b0 + bn) + [R_C], [R_PSN])
            act(LNT[:, b0:b0 + bn], PSN[:, 0:bn], AF.Ln, [R_PSN], R_LNTb.cols(b0, b0 + bn), bias=EPS,
                scale=1.0 / dsz)
            act(RS[:, b0:b0 + bn], LNT[:, b0:b0 + bn], AF.Exp, R_LNTb.cols(b0, b0 + bn), R_RSb.cols(b0, b0 + bn),
                scale=-0.5)

    def prenorm(l, i, nt):
        rms_stats(lambda kc: X[:, kc, :], RX, nt, 8, D)
        for (b0, bn) in blocks(nt):
            for kc in range(8):
                c = nwi(l, i, kc)
                stt(XN[:, kc, b0:b0 + bn], X[:, kc, b0:b0 + bn], NW[:, c:c + 1], RS[:, b0:b0 + bn], ALU.mult, ALU.mult,
                    RX[kc].cols(b0, b0 + bn) + R_RSb.cols(b0, b0 + bn) + [R_C], RXN[kc].cols(b0, b0 + bn))

    def postnorm_residual(l, i, nt, half, to_y=False):
        rms_stats(lambda kc: YB[:, kc, :], RY, nt, 8, D)
        wt = NWH if half else NW
        for bi, (b0, bn) in enumerate(blocks(nt)):
            for kc in range(8):
                c = nwi(l, i, kc)
                k = (kc * 2 + bi) % 2
                stt(TMP[k][:, 0:bn], YB[:, kc, b0:b0 + bn], wt[:, c:c + 1], RS[:, b0:b0 + bn], ALU.mult, ALU.mult,
                    RY[kc].cols(b0, b0 + bn) + R_RSb.cols(b0, b0 + bn) + [R_C], [R_TMP[k]])
                if to_y:
                    tt(YB[:, kc, b0:b0 + bn], X[:, kc, b0:b0 + bn], TMP[k][:, 0:bn], ALU.add,
                       RX[kc].cols(b0, b0 + bn) + [R_TMP[k]], RY[kc].cols(b0, b0 + bn), eng=RES_ENG)
                else:
                    tt(X[:, kc, b0:b0 + bn], X[:, kc, b0:b0 + bn], TMP[k][:, 0:bn], ALU.add,
                       RX[kc].cols(b0, b0 + bn) + [R_TMP[k]], RX[kc].cols(b0, b0 + bn), eng=RES_ENG)

    def ffn(l, f, nt):
        wgv = dr["wg"][l, f].rearrange("(kc p) n -> p kc n", p=128)
        wuv = dr["wu"][l, f].rearrange("(kc p) n -> p kc n", p=128)
        wdv = dr["wd"][l, f].rearrange("(jc p) n -> p jc n", p=128)
        blks = blocks(nt)
        for half in range(2):
            for jj in range(11):
                j = half * 11 + jj
                si = wsc[0] % 4
                wsc[0] += 1
                dma_pool(WS[si][:, :, 0:128], wgv[:, :, j * 128:(j + 1) * 128], "ws%d" % si, writes=[R_WS[si]])
                dma_pool(WS[si][:, :, 128:256], wuv[:, :, j * 128:(j + 1) * 128], "wsu%d" % si, writes=[R_WSu[si]])
                for bi, (b0, bn) in enumerate(blks):
                    pg, pu = PSA[bi % 2], PSB[bi % 2]
                    for kc in range(8):
                        mm(pg[:, 0:bn], WS[si][:, kc, 0:128], XN[:, kc, b0:b0 + bn], kc == 0, kc == 7,
                           [R_WS[si]] + RXN[kc].cols(b0, b0 + bn), [R_PSA[bi % 2]])
                    for kc in range(8):
                        mm(pu[:, 0:bn], WS[si][:, kc, 128:256], XN[:, kc, b0:b0 + bn], kc == 0, kc == 7,
                           [R_WSu[si]] + RXN[kc].cols(b0, b0 + bn), [R_PSB[bi % 2]])
                    act(SGT[bi % 2][:, 0:bn], pg[:, 0:bn], AF.Silu, [R_PSA[bi % 2]], [R_SGT[bi % 2]])
                    tt(HB[:, jj, b0:b0 + bn], SGT[bi % 2][:, 0:bn], pu[:, 0:bn], ALU.mult,
                       [R_SGT[bi % 2], R_PSB[bi % 2]], RHB[jj].cols(b0, b0 + bn))
            for m in range(8):
                di = wdc[0] % NWD
                wdc[0] += 1
                dma_pool(WD[di][:, 0:11, :], wdv[:, half * 11:(half + 1) * 11, m * 128:(m + 1) * 128], "wd%d" % di,
                         writes=[R_WD[di]])
                for bi, (b0, bn) in enumerate(blks):
                    k = (m * 2 + bi) % 2
                    py = PSC[k]
                    for jj in range(11):
                        mm(py[:, 0:bn], WD[di][:, jj, :], HB[:, jj, b0:b0 + bn], jj == 0, jj == 10,
                           [R_WD[di]] + RHB[jj].cols(b0, b0 + bn), [R_PSC[k]])
                    if half == 0:
                        cpy(YB[:, m, b0:b0 + bn], py[:, 0:bn], [R_PSC[k]], RY[m].cols(b0, b0 + bn))
                    else:
                        tt(YB[:, m, b0:b0 + bn], py[:, 0:bn], YB[:, m, b0:b0 + bn], ALU.add,
                           [R_PSC[k]] + RY[m].cols(b0, b0 + bn), RY[m].cols(b0, b0 + bn))

    def load_ws(dram_view, c0, ncols):
        si = wsc[0] % 4
        wsc[0] += 1
        dma_pool(WS[si][:, :, 0:ncols], dram_view[:, :, c0:c0 + ncols], "ws%d" % si, writes=[R_WS[si], R_WSu[si]])
        return si

    def proj_fm(si, coff, nt, epi):
        for bi, (b0, bn) in enumerate(blocks(nt)):
            k = pctr[0] % 2
            pctr[0] += 1
            ps, rps = fm_banks[0][k]
            for kc in range(8):
                mm(ps[:, 0:bn], WS[si][:, kc, coff:coff + 128], XN[:, kc, b0:b0 + bn], kc == 0, kc == 7,
                   [R_WS[si], R_WSu[si]] + RXN[kc].cols(b0, b0 + bn), [rps])
            epi(ps, rps, b0, bn)

    pctr = [0]
    fm_banks = [[(PSA[0], R_PSA[0]), (PSA[1], R_PSA[1])]]
    tm_banks = [[(PSB[0], R_PSB[0]), (PSB[1], R_PSB[1])]]

    def proj_tm_tile(s0, s1, ti, hb):
        k = pctr[0] % 2
        pctr[0] += 1
        ps, rps = tm_banks[0][k]
        for hf, si in enumerate((s0, s1)):
            for kc in range(8):
                mm(ps[:, hf * 256:(hf + 1) * 256], XN[:, kc, ti * 128:(ti + 1) * 128], WS[si][:, kc, 0:256],
                   kc == 0, kc == 7, [R_WS[si], R_WSu[si]] + RXN[kc].cols(ti * 128, ti * 128 + 128), [rps])
        cpy(VT2[hb][:, ti, :], ps[:, :], [rps], [R_VT2[hb][ti]])

    def proj_tm(s0, s1, ntile):
        for ti in range(ntile):
            proj_tm_tile(s0, s1, ti, 0)

    def out_proj(wview, nkc, nt):
        for m in range(8):
            di = wdc[0] % NWD
            wdc[0] += 1
            dma_pool(WD[di][:, 0:nkc, :], wview[:, :, m * 128:(m + 1) * 128], "wd%d" % di, writes=[R_WD[di]])
            for bi, (b0, bn) in enumerate(blocks(nt)):
                k = (m * 2 + bi) % 2
                py = PSC[k]
                for kc in range(nkc):
                    mm(py[:, 0:bn], WD[di][:, kc, :], HB[:, kc, b0:b0 + bn], kc == 0, kc == nkc - 1,
                       [R_WD[di]] + RHB[kc].cols(b0, b0 + bn), [R_PSC[k]])
                cpy(YB[:, m, b0:b0 + bn], py[:, 0:bn], [R_PSC[k]], RY[m].cols(b0, b0 + bn))

    def decay_tables(sigma, tiles, nt, rbt):
        BT = FS[:, 0, :]
        ntile = len(tiles)
        npt = sum(1 for t in tiles if t < 16)
        btv = BT[:, 0:ntile * 128].rearrange("p (a n) -> p a n", n=128)
        if npt:
            ts(BC[:, 0, 0:npt], btv[:, 0:npt, 63], -sigma, None, ALU.mult, None, [rbt], [R_BC])
            ts(BC[:, 1, 0:npt], btv[:, 0:npt, 63], sigma, None, ALU.mult, None, [rbt], [R_BC])
            ts(BC[:, 2, 0:npt], btv[:, 0:npt, 127], sigma, None, ALU.mult, None, [rbt], [R_BC])
        for ti, t in enumerate(tiles):
            cs = slice(ti * 128, (ti + 1) * 128)
            if t < 16:
                act(FS[:, 1, cs], BT[:, cs], AF.Exp, [rbt, R_BC], [R_FS[1]], bias=BC[:, 0, ti:ti + 1], scale=sigma)
                act(FS[:, 2, cs], BT[:, cs], AF.Exp, [rbt, R_BC], [R_FS[2]], bias=BC[:, 1, ti:ti + 1], scale=-sigma)
                act(FS[:, 3, cs], BT[:, cs], AF.Exp, [rbt, R_BC], [R_FS[3]], bias=BC[:, 2, ti:ti + 1], scale=-sigma)
            else:
                act(FS[:, 1, cs], BT[:, cs], AF.Exp, [rbt], [R_FS[1]], scale=sigma)
                act(FS[:, 2, cs], BT[:, cs], AF.Exp, [rbt], [R_FS[2]], scale=-sigma)
                b8 = BT[:, cs].rearrange("p (j i) -> p j i", i=8)
                ts(BC[:, 0, 0:8], b8[:, 0:8, 7], sigma, None, ALU.mult, None, [rbt], [R_BC])
                ts(BC[:, 1, 0:8], b8[:, 8:16, 7], sigma, None, ALU.mult, None, [rbt], [R_BC])
                for j in range(16):
                    c0 = ti * 128 + j * 8
                    act(FS[:, 3, c0:c0 + 8], BT[:, c0:c0 + 8], AF.Exp, [rbt, R_BC], [R_FS[3]],
                        bias=BC[:, j // 8, (j % 8):(j % 8) + 1], scale=-sigma)

    def mixer_even(g, tiles, nt):
        ewv = dr["ewin"].rearrange("(kc p) n -> p kc n", p=128)
        ntile = len(tiles)
        has_s = tiles[-1] == 16
        npt = ntile - (1 if has_s else 0)
        col0 = tiles[0] * 128
        BT = FS[:, 0, :]
        rbt = R_FS[0]
        YALL = HB
        dma_sp(SCANM[:, 0:nt], dr["scanm"][:, col0:col0 + nt], "scanm", writes=[R_SCANM])

        def epi_silu(dst, rdst):
            def f(ps, rps, b0, bn):
                act(dst[:, b0:b0 + bn], ps[:, 0:bn], AF.Silu, [rps],
                    rdst.cols(b0, b0 + bn) if hasattr(rdst, "cols") else [rdst])
            return f

        for c2 in range(2):
            si = load_ws(ewv, 1024 + c2 * 256, 256)
            for u in range(2):
                h = c2 * 2 + u
                proj_fm(si, u * 128, nt, epi_silu(YALL[:, h, :], RHB[h]))
        for mixer in (0, 1):
            if DBG < 2 or (mixer == 1 and DBG < 8):
                break
            dbg = DBG if mixer == 0 else DBG2
            if mixer == 1:
                for c2 in range(2):
                    si = load_ws(ewv, 3088 + c2 * 256, 256)
                    for u in range(2):
                        h = c2 * 2 + u
                        proj_fm(si, u * 128, nt, epi_silu(YALL[:, 4 + h, :], RHB[4 + h]))
                for c2 in range(2):
                    si = load_ws(ewv, 1552 + c2 * 256, 256)
                    for u in range(2):
                        h = c2 * 2 + u
                        proj_fm(si, u * 128, nt, epi_silu(BS[:, 4 + h, :], R_BS[4 + h]))
            nchunk = 2 if mixer == 0 else 4
            if mixer == 0:
                si = load_ws(ewv, 1536, 16)
                for bi, (b0, bn) in enumerate(blocks(nt)):
                    k = pctr[0] % 2
                    pctr[0] += 1
                    for kc in range(8):
                        mm(PSA[k][0:16, 0:bn], WS[si][:, kc, 0:16], XN[:, kc, b0:b0 + bn], kc == 0, kc == 7,
                           [R_WS[si], R_WSu[si]] + RXN[kc].cols(b0, b0 + bn), [R_PSA[k]])
                    cpy(LR[0:16, b0:b0 + bn], PSA[k][0:16, 0:bn], [R_PSA[k]], [R_LR])
            for c in range(nchunk):
                if dbg < 3:
                    break
                if mixer == 0:
                    for bi, (b0, bn) in enumerate(blocks(nt)):
                        k = pctr[0] % 2
                        pctr[0] += 1
                        mm(PSA[k][:, 0:bn], W2[0:16, c * 128:(c + 1) * 128], LR[0:16, b0:b0 + bn], True, True,
                           [R_C, R_LR], [R_PSA[k]])
                        if DBGB == 1:
                            act(FS[:, 1, b0:b0 + bn], PSA[k][:, 0:bn], AF.Exp, [R_PSA[k], R_C], [R_FS[1]], scale=-1.0)
                        else:
                            act(FS[:, 1, b0:b0 + bn], PSA[k][:, 0:bn], AF.Exp, [R_PSA[k], R_C], [R_FS[1]],
                                bias=NBG[:, c:c + 1], scale=-1.0)
                        act(FS[:, 2, b0:b0 + bn], FS[:, 1, b0:b0 + bn], AF.Ln, [R_FS[1]], [R_FS[2]], bias=1.0)
                    sigma = -1.0 / 16.0
                    graw, rgraw = FS[:, 2, :], R_FS[2]
                else:
                    si = load_ws(ewv, 2064 + c * 128, 128)

                    def epi_f(ps, rps, b0, bn, c=c):
                        act(FS[:, 1, b0:b0 + bn], ps[:, 0:bn], AF.Exp, [rps], [R_FS[1]], scale=-1.0)
                        act(FS[:, 1, b0:b0 + bn], FS[:, 1, b0:b0 + bn], AF.Ln, [R_FS[1]], [R_FS[1]], bias=1.0)
                        act(FS[:, 1, b0:b0 + bn], FS[:, 1, b0:b0 + bn], AF.Exp, [R_FS[1]], [R_FS[1]], scale=-1.0)
                        ts(FS[:, 1, b0:b0 + bn], FS[:, 1, b0:b0 + bn], OML[:, c:c + 1], LB[:, c:c + 1], ALU.mult,
                           ALU.add, [R_FS[1], R_C], [R_FS[1]])
                        act(FS[:, 2, b0:b0 + bn], FS[:, 1, b0:b0 + bn], AF.Ln, [R_FS[1]], [R_FS[2]])
                    proj_fm(si, 0, nt, epi_f)
                    ts(LNT[:, 0:nt], FS[:, 1, 0:nt], -1.0, 1.0, ALU.mult, ALU.add, [R_FS[1]], R_LNTb.all())
                    sigma = 1.0
                    graw, rgraw = FS[:, 2, :], R_FS[2]
                if dbg < 4:
                    continue
                S.op("dve", lambda e, nt=nt, graw=graw: e.tensor_tensor_scan(
                    out=BT[:, 0:nt], data0=SCANM[:, 0:nt], data1=graw[:, 0:nt], initial=0.0, op0=ALU.mult,
                    op1=ALU.add), [rgraw, R_SCANM], [rbt], cost=2.0 * nt / 960.0 + 0.1)
                if dbg < 5:
                    continue
                decay_tables(sigma, tiles, nt, rbt)
                if npt:
                    btv = BT[:, 0:ntile * 128].rearrange("p (a n) -> p a n", n=128)
                    act(EMID[mixer][:, c, 0:npt], btv[:, 0:npt, 63], AF.Exp, [rbt], [R_EMID], scale=sigma)
                    act(ELASTP[mixer][:, c, 0:npt], btv[:, 0:npt, 127], AF.Exp, [rbt], [R_EMID], scale=sigma)
                if has_s:
                    sc0 = npt * 128
                    b8 = BT[:, sc0:sc0 + 128].rearrange("p (j i) -> p j i", i=8)
                    act(EL[:, c, :], b8[:, :, 7], AF.Exp, [rbt], [R_EL[c]], scale=sigma)
                if mixer == 0:
                    sq = load_ws(ewv, c * 128, 128)

                    def epi_q(ps, rps, b0, bn, c=c):
                        for u in range(2):
                            stt(BS[:, 4 + 2 * c + u, b0:b0 + bn], ps[:, 0:bn], HM[:, u:u + 1], FS[:, 1, b0:b0 + bn],
                                ALU.mult, ALU.mult, [rps, R_FS[1], R_C], [R_BS[4 + 2 * c + u]])
                    proj_fm(sq, 0, nt, epi_q)
                    sk = load_ws(ewv, 256 + c * 128, 128)

                    def epi_k(ps, rps, b0, bn, c=c):
                        tt(BS[:, 2 + c, b0:b0 + bn], ps[:, 0:bn], FS[:, 2, b0:b0 + bn], ALU.mult, [rps, R_FS[2]],
                           [R_BS[2 + c]])
                        tt(KDB[:, c, b0:b0 + bn], ps[:, 0:bn], FS[:, 3, b0:b0 + bn], ALU.mult, [rps, R_FS[3]],
                           [R_KDB[c]])
                    proj_fm(sk, 0, nt, epi_k)
                else:
                    tt(BS[:, 4 + c, 0:nt], BS[:, 4 + c, 0:nt], FS[:, 1, 0:nt], ALU.mult, [R_BS[4 + c], R_FS[1]],
                       [R_BS[4 + c]])
                    tt(BS[:, c, 0:nt], LNT[:, 0:nt], FS[:, 2, 0:nt], ALU.mult, R_LNTb.all() + [R_FS[2]], [R_BS[c]])
                    tt(KDB[:, c, 0:nt], LNT[:, 0:nt], FS[:, 3, 0:nt], ALU.mult, R_LNTb.all() + [R_FS[3]], [R_KDB[c]])
            if dbg < 6:
                continue
            vcol = 512 if mixer == 0 else 2576
            s0 = load_ws(ewv, vcol, 256)
            s1 = load_ws(ewv, vcol + 256, 256)
            proj_tm(s0, s1, ntile)
            for ti in range(ntile):
                for c in range(nchunk):
                    tr(PSTb[:, c * 128:(c + 1) * 128], KDB[:, c, ti * 128:(ti + 1) * 128], [R_KDB[c], R_C], [R_PST])
                cpy(KDT[:, ti, 0:nchunk * 128], PSTb[:, 0:nchunk * 128], [R_PST], [R_KDT[ti]], eng="dve")
            if dbg < 7:
                continue
            QEo = 0 if mixer == 0 else 4
            KEo = 2 if mixer == 0 else 0
            STf = SA if mixer == 0 else SBs
            R_STf = R_SA if mixer == 0 else R_SB
            STb2 = SAb2 if mixer == 0 else SBb2
            R_STb2 = R_SAb2 if mixer == 0 else R_SBb2
            normw = GLANW if mixer == 0 else HGNW
            EMIDs = EMID[mixer]
            osrcs = {}

            def stage_a(ti, t, mixer=mixer, KEo=KEo):
                cs = slice(ti * 128, (ti + 1) * 128)
                kind = 0 if t < 16 else 1
                psc, rpsc = PSA[ti % 2], R_PSA[ti % 2]
                pk = ti % 2
                c0 = ti * 128
                for h in range(4):
                    c = h // 2 if mixer == 0 else h
                    mm(psc[:, h * 128 + 64:h * 128 + 128], BS[:, KEo + c, cs], BS[:, 4 + h, c0 + 64:c0 + 128], True, True,
                       [R_BS[KEo + c], R_BS[4 + h]], [rpsc])
                    mm(psc[0:64, h * 128:h * 128 + 64], BS[:, KEo + c, c0:c0 + 64], BS[:, 4 + h, c0:c0 + 64], True, True,
                       [R_BS[KEo + c], R_BS[4 + h]], [rpsc])
                for h in range(4):
                    tt(PT[pk][:, h * 128 + 64:h * 128 + 128], psc[:, h * 128 + 64:h * 128 + 128], MASK0[:, kind, 64:128],
                       ALU.mult, [rpsc, R_C], [R_PT[pk]])
                    tt(PT[pk][0:64, h * 128:h * 128 + 64], psc[0:64, h * 128:h * 128 + 64], MASK0[0:64, kind, 0:64],
                       ALU.mult, [rpsc, R_C], [R_PT[pk]])

            def stage_b(ti, t, mixer=mixer, nchunk=nchunk, QEo=QEo):
                cs = slice(ti * 128, (ti + 1) * 128)
                kind = 0 if t < 16 else 1
                pso, rpso = PSB[ti % 2], R_PSB[ti % 2]
                pk = ti % 2
                stb, rstb = STb2[ti % 2], R_STb2[ti % 2]
                use_inter = (t > 0) and (t < 16)
                if use_inter:
                    for c in range(nchunk):
                        ts(stb[:, c, :], STf[:, c, :], EMIDs[:, c, ti:ti + 1], None, ALU.mult, None, [R_STf, R_EMID],
                           [rstb])
                pkv, rpkv = PSC[ti % 2], R_PSC[ti % 2]
                if kind == 0:
                    if mixer == 0:
                        for c in range(2):
                            mm(pkv[:, c * 256:(c + 1) * 256], KDT[:, ti, c * 128:(c + 1) * 128],
                               VT[:, ti, c * 256:(c + 1) * 256], True, True, [R_KDT[ti], R_VT[ti]], [rpkv])
                    else:
                        for h in range(4):
                            mm(pkv[:, h * 128:(h + 1) * 128], KDT[:, ti, h * 128:(h + 1) * 128],
                               VT[:, ti, h * 128:(h + 1) * 128], True, True, [R_KDT[ti], R_VT[ti]], [rpkv])
                for h in range(4):
                    c = h // 2 if mixer == 0 else h
                    last = not use_inter
                    mm(pso[:, h * 128:(h + 1) * 128], VT[:, ti, h * 128:(h + 1) * 128], PT[pk][:, h * 128:(h + 1) * 128],
                       True, last, [R_VT[ti], R_PT[pk]], [rpso])
                    if use_inter:
                        mm(pso[:, h * 128:(h + 1) * 128], stb[:, c, :], BS[:, 4 + h, cs], False, True,
                           [rstb, R_BS[4 + h]], [rpso])
                osrcs[ti] = (pso, rpso)
                if kind == 1:
                    osrcs[ti] = sample_even(mixer, ti, pso, rpso, nchunk, QEo)
                else:
                    if mixer == 0:
                        for h in range(4):
                            c, pr = h // 2, slice((h % 2) * 64, (h % 2) * 64 + 64)
                            src = pkv[pr, c * 256 + (h % 2) * 128: c * 256 + (h % 2) * 128 + 128]
                            if t == 0:
                                cpy(STf[pr, c, :], src, [rpkv], [R_STf], eng="dve")
                            else:
                                stt(STf[pr, c, :], STf[pr, c, :], ELASTP[mixer][pr, c, ti:ti + 1], src, ALU.mult,
                                    ALU.add, [R_STf, rpkv, R_EMID], [R_STf])
                    else:
                        for h in range(4):
                            src = pkv[:, h * 128:(h + 1) * 128]
                            if t == 0:
                                cpy(STf[:, h, :], src, [rpkv], [R_STf], eng="dve")
                            else:
                                stt(STf[:, h, :], STf[:, h, :], ELASTP[mixer][:, h, ti:ti + 1], src, ALU.mult, ALU.add,
                                    [R_STf, rpkv, R_EMID], [R_STf])

            def stage_c(ti, t, mixer=mixer):
                cs = slice(ti * 128, (ti + 1) * 128)
                osrc, rosrc = osrcs[ti]
                act(SQ[:, :], osrc[:, :], AF.Square, [rosrc], [R_SQ])
                for h in range(4):
                    mm(PSN[:, h * 128:(h + 1) * 128], ONES[:, :], SQ[:, h * 128:(h + 1) * 128], True, True,
                       [R_SQ, R_C], [R_PSN])
                act(TMP[0][:, :], PSN[:, :], AF.Ln, [R_PSN], [R_TMP[0]], bias=EPS, scale=1.0 / 128.0)
                act(TMP[0][:, :], TMP[0][:, :], AF.Exp, [R_TMP[0]], [R_TMP[0]], scale=-0.5)
                stt(TMP[1][:, :], osrc[:, :], normw[:, 0:1], TMP[0][:, :], ALU.mult, ALU.mult, [rosrc, R_TMP[0], R_C],
                    [R_TMP[1]])
                yo = mixer * 4
                rh = [r for i in range(4) for r in RHB[yo + i].cols(ti * 128, ti * 128 + 128)]
                tt(YALL[:, yo:yo + 4, cs], TMP[1][:, :].rearrange("p (a n) -> p a n", a=4), YALL[:, yo:yo + 4, cs],
                   ALU.mult, [R_TMP[1]] + rh, rh)

            stage_a(0, tiles[0])
            for ti, t in enumerate(tiles):
                if ti + 1 < ntile:
                    stage_a(ti + 1, tiles[ti + 1])
                stage_b(ti, t)
                if ti >= 1:
                    stage_c(ti - 1, tiles[ti - 1])
            stage_c(ntile - 1, tiles[-1])
            if g == 2:
                if mixer == 0:
                    dma_sp(dr["gla_p"].rearrange("(c u) d v -> (u d) c v", u=2), SA[:, :, :], "stout", reads=[R_SA])
                else:
                    dma_sp(dr["hgrn_p"].rearrange("h d v -> d h v"), SBs[:, :, :], "stout", reads=[R_SB])

    EMID = [sb("EMID%d" % i, [128, 4, 8], F32) for i in range(2)]
    ELASTP = [sb("ELASTP%d" % i, [128, 4, 8], F32) for i in range(2)]
    R_EMID = R("EMID")

    OS = sb("OS", [128, 512], F32)
    R_OS = R("OS")

    def sample_even(mixer, ti, pso, rpso, nchunk, QEo):
        cs0 = ti * 128
        psi, rpsi = PSA[(ti + 1) % 2], R_PSA[(ti + 1) % 2]

        def load(j):
            k = j % NS0
            if mixer == 0:
                src = dr["sgla"][j].rearrange("(c u) d v -> (u d) c v", u=2)
                dma_sp(S0F[k][:, 0, 0:256].rearrange("p (c v) -> p c v", c=2), src, "s0f%d" % k, writes=[R_S0F[k]])
            else:
                src = dr["shg"][j].rearrange("h d v -> d h v")
                dma_sp(S0F[k][:, 0, 0:512].rearrange("p (c v) -> p c v", c=4), src, "s0f%d" % k, writes=[R_S0F[k]])

        load(0)
        load(1)
        for j in range(16):
            k = j % NS0
            k2 = j % 2
            if j + 2 < 16:
                load(j + 2)
            ncol = 256 if mixer == 0 else 512
            cpy(S0B[k][:, 0, 0:ncol], S0F[k][:, 0, 0:ncol], [R_S0F[k]], [R_S0B[k]])
            for h in range(4):
                if mixer == 0:
                    c, pr = h // 2, slice((h % 2) * 64, (h % 2) * 64 + 64)
                else:
                    c, pr = h, slice(0, 128)
                mm(psi[:, h * 128 + j * 8:h * 128 + j * 8 + 8], S0B[k][:, 0, c * 128:(c + 1) * 128],
                   BS[:, 4 + h, cs0 + j * 8:cs0 + j * 8 + 8], True, True, [R_S0B[k], R_BS[4 + h]], [rpsi])
            if DBGB == 4 and mixer == 1:
                continue
            ts(VX[k2][:, :], VT[:, ti, :], BM[:, j:j + 1], None, ALU.mult, None, [R_VT[ti], R_C], [R_VX[k2]])
            pkv, rpkv = PSC[k2], R_PSC[k2]
            if mixer == 0:
                for c in range(2):
                    mm(pkv[:, c * 256:(c + 1) * 256], KDT[:, ti, c * 128:(c + 1) * 128], VX[k2][:, c * 256:(c + 1) * 256],
                       True, True, [R_KDT[ti], R_VX[k2]], [rpkv])
                for h in range(4):
                    c, pr = h // 2, slice((h % 2) * 64, (h % 2) * 64 + 64)
                    src = pkv[pr, c * 256 + (h % 2) * 128: c * 256 + (h % 2) * 128 + 128]
                    stt(S0F[k][pr, 0, c * 128:(c + 1) * 128], S0F[k][pr, 0, c * 128:(c + 1) * 128], EL[pr, c, j:j + 1],
                        src, ALU.mult, ALU.add, [R_S0F[k], rpkv, R_EL[c]], [R_S0F[k]])
                dma_sp(dr["gla_s"][j].rearrange("(c u) d v -> (u d) c v", u=2),
                       S0F[k][:, 0, 0:256].rearrange("p (c v) -> p c v", c=2), "s0o%d" % k, reads=[R_S0F[k]])
            else:
                for h in range(4):
                    mm(pkv[:, h * 128:(h + 1) * 128], KDT[:, ti, h * 128:(h + 1) * 128], VX[k2][:, h * 128:(h + 1) * 128],
                       True, True, [R_KDT[ti], R_VX[k2]], [rpkv])
                for h in range(4):
                    stt(S0F[k][:, 0, h * 128:(h + 1) * 128], S0F[k][:, 0, h * 128:(h + 1) * 128], EL[:, h, j:j + 1],
                        pkv[:, h * 128:(h + 1) * 128], ALU.mult, ALU.add, [R_S0F[k], rpkv, R_EL[h]], [R_S0F[k]])
                for h in range(4):
                    dma_sp(dr["hgrn_s"][j, h], S0F[k][:, 0, h * 128:(h + 1) * 128], "s0o%d" % k, reads=[R_S0F[k]])
        cpy(OS[:, :], pso[:, :], [rpso], [R_OS])
        tt(OS[:, :], OS[:, :], psi[:, :], ALU.add, [R_OS, rpsi], [R_OS])
        return OS, R_OS

    def mixer_odd(g, tiles, nt):
        owv = dr["owin"].rearrange("(kc p) n -> p kc n", p=128)
        ntile = len(tiles)
        col0 = tiles[0] * 128
        YALL = HB
        dma_sp(ROPE[:, :, 0:nt], dr["rope"][:, :, col0:col0 + nt], "rope", writes=R_KDB)
        COS, SIN = ROPE[:, 0, :], ROPE[:, 1, :]
        fm_banks[0] = [(PSA[1], R_PSA[1]), (PSC[1], R_PSC[1])]
        tm_banks[0] = [(PSA[1], R_PSA[1]), (PSC[1], R_PSC[1])]

        def proj_gen(h):
            hb = h % 2
            qo, ko = 4 * hb, 4 * hb + 2
            dma_sp(DQT[hb][:, :, :], dr["dqt"][:, h], "dq%d" % hb, writes=[R_DQ[hb]])
            dma_sp(DMT[hb][:, :, :], dr["dmt"][:, h], "dq%d" % hb, writes=[R_DQ[hb]])
            for c2 in range(2):
                si = load_ws(owv, 4096 + h * 512 + c2 * 256, 256)
                for u in range(2):
                    d = c2 * 2 + u

                    def epi_g(ps, rps, b0, bn, d=d):
                        act(YALL[:, h * 4 + d, b0:b0 + bn], ps[:, 0:bn], AF.Silu, [rps],
                            RHB[h * 4 + d].cols(b0, b0 + bn))
                    proj_fm(si, u * 128, nt, epi_g)
            yield
            for which in range(2):
                si = load_ws(owv, which * 1024 + h * 256, 256)
                for u in range(2):
                    def epi_x(ps, rps, b0, bn, u=u):
                        cpy(FS[:, u, b0:b0 + bn], ps[:, 0:bn], [rps], [R_FS[u]])
                    proj_fm(si, u * 128, nt, epi_x)
                yield
                tt(FS[:, 2, 0:nt], FS[:, 0, 0:nt], COS[:, 0:nt], ALU.mult, [R_FS[0]] + R_KDB, [R_FS[2]])
                tt(FS[:, 3, 0:nt], FS[:, 1, 0:nt], SIN[:, 0:nt], ALU.mult, [R_FS[1]] + R_KDB, [R_FS[3]])
                tt(FS[:, 2, 0:nt], FS[:, 2, 0:nt], FS[:, 3, 0:nt], ALU.subtract, [R_FS[2], R_FS[3]], [R_FS[2]])
                tt(FS[:, 3, 0:nt], FS[:, 0, 0:nt], SIN[:, 0:nt], ALU.mult, [R_FS[0]] + R_KDB, [R_FS[3]])
                tt(FS[:, 0, 0:nt], FS[:, 1, 0:nt], COS[:, 0:nt], ALU.mult, [R_FS[1]] + R_KDB, [R_FS[0]])
                tt(FS[:, 3, 0:nt], FS[:, 3, 0:nt], FS[:, 0, 0:nt], ALU.add, [R_FS[3], R_FS[0]], [R_FS[3]])
                yield
                if which == 0:
                    for ti, t in enumerate(tiles):
                        cs = slice(ti * 128, (ti + 1) * 128)
                        kind = 0 if t < 16 else 1
                        for u in range(2):
                            tt(BS[:, qo + u, cs], FS[:, 2 + u, cs], DQT[hb][:, kind, :], ALU.mult,
                               [R_FS[2 + u], R_DQ[hb]], [R_BS[qo + u]])
                else:
                    for u in range(2):
                        cpy(BS[:, ko + u, 0:nt], FS[:, 2 + u, 0:nt], [R_FS[2 + u]], [R_BS[ko + u]], eng="dve")
                yield
            s0 = load_ws(owv, 2048 + h * 512, 256)
            s1 = load_ws(owv, 2048 + h * 512 + 256, 256)
            for ti in range(ntile):
                proj_tm_tile(s0, s1, ti, hb)
                if ti % 2 == 1:
                    yield

        def attn_gen(h):
            hb = h % 2
            qo, ko = 4 * hb, 4 * hb + 2
            VTh, R_VTh = VT2[hb], R_VT2[hb]
            kc0 = hb * 256
            R_KDTh = R_KDT2[hb]
            if g > 0:
                dma_sp(SR[:, :, :], dr["scr_ret"][h].rearrange("(c p) v -> p c v", p=128), "srld", reads=[R_scr[h]],
                       writes=[R_SR])
                cpy(SRB[0][:, :, :], SR[:, :, :], [R_SR], [R_SRB[0]])
            sbi = [0]
            osrcs = {}

            def stage_a(ti, t):
                cs = slice(ti * 128, (ti + 1) * 128)
                kind = 0 if t < 16 else 1
                psc, rpsc = PSA[0], R_PSA[0]
                pk = ti % 2
                for u in range(2):
                    tr(PSTb[:, u * 128:(u + 1) * 128], BS[:, ko + u, cs], [R_BS[ko + u], R_C], [R_PST])
                ts(KDT[:, ti, kc0:kc0 + 256], PSTb[:, 0:256], DKC[:, h, kind:kind + 1], None, ALU.mult, None,
                   [R_PST, R_C], [R_KDTh[ti]])
                for u in range(2):
                    mm(psc[:, 0:128], BS[:, ko + u, cs], BS[:, qo + u, cs], u == 0, u == 1,
                       [R_BS[ko + u], R_BS[qo + u]], [rpsc])
                tt(PT[pk][:, 0:128], psc[:, 0:128], DMT[hb][:, kind, :], ALU.mult, [rpsc, R_DQ[hb]], [R_PT[pk]])

            def stage_b(ti, t):
                cs = slice(ti * 128, (ti + 1) * 128)
                kind = 0 if t < 16 else 1
                pso, rpso = PSB[ti % 2], R_PSB[ti % 2]
                pk = ti % 2
                use_inter = (t > 0) and (t < 16)
                cur = sbi[0]
                if kind == 0:
                    for u in range(2):
                        pkv, rpkv = PSC[0], R_PSC[0]
                        mm(pkv[:, :], KDT[:, ti, kc0 + u * 128:kc0 + (u + 1) * 128], VTh[:, ti, :], True, True,
                           [R_KDTh[ti], R_VTh[ti]], [rpkv])
                        if t == 0:
                            cpy(SR[:, u, :], pkv[:, :], [rpkv], [R_SR], eng="dve")
                        else:
                            stt(SR[:, u, :], SR[:, u, :], float(GAMMA[h] ** 128), pkv[:, :], ALU.mult, ALU.add,
                                [R_SR, rpkv], [R_SR])
                    cpy(SRB[1 - cur][:, :, :], SR[:, :, :], [R_SR], [R_SRB[1 - cur]])
                    sbi[0] = 1 - cur
                for d in range(4):
                    last = not use_inter
                    mm(pso[:, d * 128:(d + 1) * 128], VTh[:, ti, d * 128:(d + 1) * 128], PT[pk][:, 0:128], True, last,
                       [R_VTh[ti], R_PT[pk]], [rpso])
                    if use_inter:
                        for u in range(2):
                            mm(pso[:, d * 128:(d + 1) * 128], SRB[cur][:, u, d * 128:(d + 1) * 128], BS[:, qo + u, cs],
                               False, u == 1, [R_SRB[cur], R_BS[qo + u]], [rpso])
                osrcs[ti] = (pso, rpso)

            def stage_c(ti, t):
                cs = slice(ti * 128, (ti + 1) * 128)
                osrc, rosrc = osrcs[ti]
                act(SQ[:, :], osrc[:, :], AF.Square, [rosrc], [R_SQ])
                for d in range(4):
                    mm(PSN[:, 0:128], ONES[:, :], SQ[:, d * 128:(d + 1) * 128], d == 0, d == 3, [R_SQ, R_C], [R_PSN])
                act(RT[:, :], PSN[:, 0:128], AF.Ln, [R_PSN], [R_RT], bias=EPS, scale=1.0 / 512.0)
                act(RT[:, :], RT[:, :], AF.Exp, [R_RT], [R_RT], scale=-0.5)
                for d in range(4):
                    stt(TMP[1][:, d * 128:(d + 1) * 128], osrc[:, d * 128:(d + 1) * 128], RETNW[:, d:d + 1], RT[:, :],
                        ALU.mult, ALU.mult, [rosrc, R_RT, R_C], [R_TMP[1]])
                rh = [r for i in range(4) for r in RHB[h * 4 + i].cols(ti * 128, ti * 128 + 128)]
                tt(YALL[:, h * 4:h * 4 + 4, cs], TMP[1][:, :].rearrange("p (a n) -> p a n", a=4),
                   YALL[:, h * 4:h * 4 + 4, cs], ALU.mult, [R_TMP[1]] + rh, rh)

            stage_a(0, tiles[0])
            yield
            for ti, t in enumerate(tiles):
                if ti + 1 < ntile:
                    stage_a(ti + 1, tiles[ti + 1])
                stage_b(ti, t)
                yield
                if t == 16:
                    for _ in sample_odd_gen(h, ti, osrcs):
                        yield
                if ti >= 1:
                    stage_c(ti - 1, tiles[ti - 1])
                    yield
            stage_c(ntile - 1, tiles[-1])
            for u in range(2):
                if g < 2:
                    dma_sp(dr["scr_ret"][h, u * 128:(u + 1) * 128, :], SR[:, u, :], "srst%d" % h, reads=[R_SR],
                           writes=[R_scr[h]])
                else:
                    dma_sp(dr["ret_p"][h, u * 128:(u + 1) * 128, :], SR[:, u, :], "stout", reads=[R_SR])
            yield

        gens = [proj_gen(h) for h in range(4)]
        for _ in gens[0]:
            pass
        for h in range(4):
            pg = gens[h + 1] if h + 1 < 4 else None
            for _ in attn_gen(h):
                if pg is not None:
                    try:
                        next(pg)
                    except StopIteration:
                        pg = None
            if pg is not None:
                for _ in pg:
                    pass
        fm_banks[0] = [(PSA[0], R_PSA[0]), (PSA[1], R_PSA[1])]
        tm_banks[0] = [(PSB[0], R_PSB[0]), (PSB[1], R_PSB[1])]

    def sample_odd_gen(h, ti, osrcs):
        hb = h % 2
        qo = 4 * hb
        VTh, R_VTh = VT2[hb], R_VT2[hb]
        kc0 = hb * 256
        R_KDTh = R_KDT2[hb]
        pso, rpso = osrcs[ti]
        cs0 = ti * 128
        g8 = float(GAMMA[h] ** 8)
        psi, rpsi = PSA[0], R_PSA[0]

        def load(j):
            k = j % NS0
            dma_sp(S0F[k][:, :, :], dr["sret"][j, h].rearrange("(c p) v -> p c v", p=128), "s0f%d" % k,
                   writes=[R_S0F[k]])

        load(0)
        load(1)
        for j in range(16):
            k = j % NS0
            k2 = j % 2
            if j + 2 < 16:
                load(j + 2)
            cpy(S0B[k][:, :, :], S0F[k][:, :, :], [R_S0F[k]], [R_S0B[k]])
            for d in range(4):
                for u in range(2):
                    mm(psi[:, d * 128 + j * 8:d * 128 + j * 8 + 8], S0B[k][:, u, d * 128:(d + 1) * 128],
                       BS[:, qo + u, cs0 + j * 8:cs0 + j * 8 + 8], u == 0, u == 1, [R_S0B[k], R_BS[qo + u]], [rpsi])
            ts(VX[k2][:, :], VTh[:, ti, :], BM[:, j:j + 1], None, ALU.mult, None, [R_VTh[ti], R_C], [R_VX[k2]])
            for u in range(2):
                pkv, rpkv = PSC[0], R_PSC[0]
                mm(pkv[:, :], KDT[:, ti, kc0 + u * 128:kc0 + (u + 1) * 128], VX[k2][:, :], True, True,
                   [R_KDTh[ti], R_VX[k2]], [rpkv])
                stt(S0F[k][:, u, :], S0F[k][:, u, :], g8, pkv[:, :], ALU.mult, ALU.add, [R_S0F[k], rpkv],
                    [R_S0F[k]])
            for u in range(2):
                dma_sp(dr["ret_s"][j, h, u * 128:(u + 1) * 128, :], S0F[k][:, u, :], "s0o%d" % k, reads=[R_S0F[k]])
            if j % 2 == 1:
                yield
        cpy(OS[:, :], pso[:, :], [rpso], [R_OS])
        tt(OS[:, :], OS[:, :], psi[:, :], ALU.add, [R_OS, rpsi], [R_OS])
        osrcs[ti] = (OS, R_OS)
        yield

    xv = dr["xT"].rearrange("(kc p) t -> p kc t", p=128)
    yv = dr["yT"].rearrange("(kc p) t -> p kc t", p=128)
    for g, tiles in enumerate(GROUP_TILES):
        nt = len(tiles) * 128
        col0 = tiles[0] * 128
        dma_sp(X[:, :, 0:nt], xv[:, :, col0:col0 + nt], "xin", writes=flat(RX))
        for l in range(2):
            st = l * 10
            if stop > st + 0:
                prenorm(l, 0, nt)
                ffn(l, 0, nt)
                postnorm_residual(l, 1, nt, True)
            if stop > st + 1:
                prenorm(l, 2, nt)
                S.barrier()
                if l == 0:
                    mixer_even(g, tiles, nt)
                else:
                    mixer_odd(g, tiles, nt)
                S.barrier()
                if l == 0:
                    out_proj(dr["ewout"].rearrange("(kc p) n -> p kc n", p=128), 8, nt)
                else:
                    out_proj(dr["owout"].rearrange("(kc p) n -> p kc n", p=128), 16, nt)
                postnorm_residual(l, 3, nt, False)
            if stop > st + 2:
                prenorm(l, 4, nt)
                ffn(l, 1, nt)
                postnorm_residual(l, 5, nt, True, to_y=(l == 1))
        if stop > 12:
            dma_sp(yv[:, :, col0:col0 + nt], YB[:, :, 0:nt], "yout", reads=flat(RY))
        else:
            dma_sp(yv[:, :, col0:col0 + nt], X[:, :, 0:nt], "yout", reads=flat(RX))

    S.finalize()
    with nc.Block() as block:
        @block.tensor
        def _(e):
            S.emit("pe", e)

        @block.scalar
        def _(e):
            S.emit("act", e)

        @block.vector
        def _(e):
            S.emit("dve", e)

        @block.gpsimd
        def _(e):
            S.emit("pool", e)

        @block.sync
        def _(e):
            S.emit("sp", e)
    return nc


_NC_CACHE = {}


def _get_nc(stop=99):
    if stop not in _NC_CACHE:
        _NC_CACHE[stop] = build(stop)
    return _NC_CACHE[stop]


def make_in_maps(x_prompt, x_sample, state_gla, state_hgrn, state_ret, norm_w, ffn_w_gate, ffn_w_up, ffn_w_down,
                 even_w_in, gla_w_gate2, gla_b_gate, gla_norm_w, hgrn_lb_table, hgrn_norm_w, even_w_out, odd_w_in,
                 ret_norm_w, odd_w_out):
    f = lambda a: np.ascontiguousarray(np.asarray(a, dtype=np.float32))
    consts = _consts()
    shared = dict(
        nw=f(np.asarray(norm_w).reshape(12, 8, 128).transpose(2, 0, 1).reshape(128, 96)),
        wg=f(ffn_w_gate), wu=f(ffn_w_up), wd=f(ffn_w_down),
        ewin=f(np.asarray(even_w_in)[0]), w2=f(np.asarray(gla_w_gate2)[0]),
        bg=f(np.asarray(gla_b_gate)[0].reshape(2, 128).T), glanw=f(np.asarray(gla_norm_w)[0].reshape(128, 1)),
        lbt=f(np.asarray(hgrn_lb_table).reshape(3, 4, 128).transpose(2, 0, 1)),
        hgnw=f(np.asarray(hgrn_norm_w)[0].reshape(128, 1)), ewout=f(np.asarray(even_w_out)[0]),
        owin=f(np.asarray(odd_w_in)[0]), retnw=f(np.asarray(ret_norm_w)[0].reshape(4, 128).T),
        owout=f(np.asarray(odd_w_out)[0]),
    )
    for k, v in consts.items():
        shared["c_" + k] = f(v)
    xp = np.asarray(x_prompt, dtype=np.float32)
    xs = np.asarray(x_sample, dtype=np.float32)
    maps = []
    for c in range(NCORE):
        m = dict(shared)
        m["xT"] = f(np.concatenate([xp[c].T, xs[16 * c:16 * c + 16].reshape(128, D).T], axis=1))
        m["sgla"] = f(np.asarray(state_gla)[0, 16 * c:16 * c + 16])
        m["shg"] = f(np.asarray(state_hgrn)[0, 16 * c:16 * c + 16])
        m["sret"] = f(np.asarray(state_ret)[0, 16 * c:16 * c + 16])
        maps.append(m)
    return maps


def assemble(results):
    y_p = np.empty((8, 2048, D), np.float32)
    y_s = np.empty((128, 8, D), np.float32)
    gla_p = np.empty((1, 8, 4, 64, 128), np.float32)
    hg_p = np.empty((1, 8, 4, 128, 128), np.float32)
    ret_p = np.empty((1, 8, 4, 256, 512), np.float32)
    gla_s = np.empty((1, 128, 4, 64, 128), np.float32)
    hg_s = np.empty((1, 128, 4, 128, 128), np.float32)
    ret_s = np.empty((1, 128, 4, 256, 512), np.float32)
    for c, r in enumerate(results):
        yT = np.asarray(r["yT"])
        y_p[c] = yT[:, :2048].T
        y_s[16 * c:16 * c + 16] = yT[:, 2048:].T.reshape(16, 8, D)
        gla_p[0, c] = r["gla_p"]
        hg_p[0, c] = r["hgrn_p"]
        ret_p[0, c] = r["ret_p"]
        gla_s[0, 16 * c:16 * c + 16] = r["gla_s"]
        hg_s[0, 16 * c:16 * c + 16] = r["hgrn_s"]
        ret_s[0, 16 * c:16 * c + 16] = r["ret_s"]
    return (y_p, y_s, gla_p, hg_p, ret_p, gla_s, hg_s, ret_s)


def kernel(**inputs):
    nc = _get_nc()
    maps = make_in_maps(**inputs)
    res = run_bass_kernel_spmd(nc, maps, core_ids=list(range(NCORE)))
    return assemble(res.results)
```
